# Optimizing a Trainium2 kernel written in Bass

```python
import math
import jax, jax.numpy as jnp
from jax import lax
import numpy as np

D_MODEL = 1024
BATCH = 1
SEQ = 16384
DEPTH = 2
DEC_BATCH = 8
DEC_SEQ = 64
PAST_LEN = 4096

CHUNK = 64
QBLOCK = 128
N_BRANCH = 4
N_HEADS = 4
HEAD_DIM = 64
BRANCH_WIDTH = N_HEADS * HEAD_DIM
MLA_D_C = 128
MLA_D_NOPE = 64
MLA_D_ROPE = 32
MLA_D_V = 64
MLA_THETA = 10000.0
ROPE_THETA = 500000.0
BAND_LEFT_CHUNKS = 8
BAND_WINDOW = BAND_LEFT_CHUNKS * CHUNK
REL_CLIP = 128
IDX_HEADS = 8
IDX_DIM = 64
DSA_TOPK = 256
D_FF = ((8 * D_MODEL // 3 + 255) // 256) * 256
ALPHA = (2 * DEPTH) ** 0.25
BETA = (8 * DEPTH) ** -0.25
NEG_INF = -1e30
LN_EPS = 1e-5
IN_SIZES = (N_HEADS * (MLA_D_NOPE + MLA_D_ROPE), MLA_D_C, MLA_D_ROPE,
            3 * BRANCH_WIDTH, 3 * BRANCH_WIDTH, 3 * BRANCH_WIDTH,
            IDX_HEADS * IDX_DIM, IDX_DIM, IDX_HEADS, N_BRANCH * D_MODEL)
IN_WIDTH = sum(IN_SIZES)

kernel_name = 'hybrid_streaming_encoder_step'


def split_cols(proj):
    offs = np.cumsum(IN_SIZES)[:-1].tolist()
    return jnp.split(proj, offs, axis=-1)


def layer_norm(x, g, b):
    xf = x.astype(jnp.float32)
    mu = jnp.mean(xf, axis=-1, keepdims=True)
    var = jnp.mean(jnp.square(xf - mu), axis=-1, keepdims=True)
    return ((xf - mu) * lax.rsqrt(var + LN_EPS)).astype(x.dtype) * g + b


def rms_norm(x, g):
    xf = x.astype(jnp.float32)
    return (xf * lax.rsqrt(jnp.mean(jnp.square(xf), axis=-1, keepdims=True) + LN_EPS)).astype(x.dtype) * g


def rope(x, pos, theta):
    r = x.shape[-1]
    half = r // 2
    inv = jnp.exp(jnp.arange(half, dtype=jnp.float32) * (-2.0 * math.log(theta) / r))
    ang = pos.astype(jnp.float32)[:, None] * inv[None, :]
    cos = jnp.cos(ang)[None, :, None, :].astype(x.dtype)
    sin = jnp.sin(ang)[None, :, None, :].astype(x.dtype)
    x1, x2 = x[..., :half], x[..., half:]
    return jnp.concatenate([x1 * cos - x2 * sin, x2 * cos + x1 * sin], axis=-1)


def partial_rope(x, pos):
    r = x.shape[-1] // 4
    return jnp.concatenate([rope(x[..., :r], pos, ROPE_THETA), x[..., r:]], axis=-1)


def chunk_visible(q_pos, k_pos):
    return (k_pos[None, :] // CHUNK) <= (q_pos[:, None] // CHUNK)


def map_query_blocks(fn, q_pos, *q_arrays):
    sq = q_pos.shape[0]
    qb = QBLOCK if sq % QBLOCK == 0 else sq
    nb = sq // qb

    def to_blocks(a):
        return jnp.moveaxis(a.reshape((a.shape[0], nb, qb) + a.shape[2:]), 1, 0)

    out = lax.map(fn, (q_pos.reshape(nb, qb),) + tuple(to_blocks(a) for a in q_arrays))
    out = jnp.moveaxis(out, 0, 1)
    return out.reshape((out.shape[0], sq) + out.shape[3:])


def mla_attention(q_nope, q_rope, k_nope, k_rope, v, q_pos, k_pos):
    scale = (MLA_D_NOPE + MLA_D_ROPE) ** -0.5

    def blk(args):
        qp, qn, qr = args
        s = (jnp.einsum('bqhd,bkhd->bhqk', qn, k_nope)
             + jnp.einsum('bqhr,bkr->bhqk', qr, k_rope)).astype(jnp.float32) * scale
        s = jnp.where(chunk_visible(qp, k_pos)[None, None], s, NEG_INF)
        p = jax.nn.softmax(s, axis=-1).astype(v.dtype)
        return jnp.einsum('bhqk,bkhd->bqhd', p, v)

    return map_query_blocks(blk, q_pos, q_nope, q_rope)


def stick_breaking_attention(q, k, v, q_pos, k_pos):
    scale = HEAD_DIM ** -0.5

    def blk(args):
        qp, qq = args
        z = jnp.einsum('bqhd,bkhd->bhqk', qq, k).astype(jnp.float32) * scale
        strict = (k_pos[None, :] < qp[:, None])[None, None]
        log_1m = jnp.where(strict, jax.nn.log_sigmoid(-z), 0.0)
        after = lax.cumsum(log_1m, axis=3, reverse=True) - log_1m
        a = jnp.where(strict, jnp.exp(jax.nn.log_sigmoid(z) + after), 0.0).astype(v.dtype)
        return jnp.einsum('bhqk,bkhd->bqhd', a, v)

    return map_query_blocks(blk, q_pos, q)


def band_attention(qc, kb, vb, q_pos, k_pos, rel_bias):
    s = jnp.einsum('bnqhd,bnkhd->bnhqk', qc, kb).astype(jnp.float32) * HEAD_DIM ** -0.5
    rel = jnp.clip(q_pos[:, :, None] - k_pos[:, None, :], -REL_CLIP, REL_CLIP) + REL_CLIP
    bias = jnp.moveaxis(rel_bias[:, rel], 0, 1).astype(jnp.float32)
    qch = q_pos[:, :, None] // CHUNK
    kch = k_pos[:, None, :] // CHUNK
    ok = (k_pos[:, None, :] >= 0) & (kch <= qch) & (kch >= qch - BAND_LEFT_CHUNKS)
    s = jnp.where(ok[None, :, None], s + bias[None], NEG_INF)
    p = jax.nn.softmax(s, axis=-1).astype(vb.dtype)
    return jnp.einsum('bnhqk,bnkhd->bnqhd', p, vb)


def band_prompt(q, kv, rel_bias):
    bsz, s_len = q.shape[0], q.shape[1]
    nc = s_len // CHUNK
    pad = BAND_LEFT_CHUNKS * CHUNK
    qc = q.reshape(bsz, nc, CHUNK, N_HEADS, HEAD_DIM)
    kvp = jnp.pad(kv, ((0, 0), (pad, 0), (0, 0), (0, 0), (0, 0)))
    kvp = kvp.reshape(bsz, nc + BAND_LEFT_CHUNKS, CHUNK, 2, N_HEADS, HEAD_DIM)
    kvb = jnp.concatenate([kvp[:, i:i + nc] for i in range(BAND_LEFT_CHUNKS + 1)], axis=2)
    pp = jnp.arange(-pad, s_len, dtype=jnp.int32).reshape(nc + BAND_LEFT_CHUNKS, CHUNK)
    kpos = jnp.concatenate([pp[i:i + nc] for i in range(BAND_LEFT_CHUNKS + 1)], axis=1)
    qpos = jnp.arange(s_len, dtype=jnp.int32).reshape(nc, CHUNK)
    out = band_attention(qc, kvb[:, :, :, 0], kvb[:, :, :, 1], qpos, kpos, rel_bias)
    return out.reshape(bsz, s_len, N_HEADS, HEAD_DIM)


def dsa_attention(q, q_idx, w_idx, kv, k_idx, q_pos, k_pos):
    n_sel = min(DSA_TOPK, k_pos.shape[0] // 4)
    scale = HEAD_DIM ** -0.5

    def blk(args):
        qp, qq, qi, wi = args
        rel = jax.nn.relu(jnp.einsum('bqjd,bkd->bqjk', qi, k_idx).astype(jnp.float32) * IDX_DIM ** -0.5)
        score = jnp.einsum('bqj,bqjk->bqk', wi.astype(jnp.float32) * IDX_HEADS ** -0.5, rel)
        vis = chunk_visible(qp, k_pos)
        score = jnp.where(vis[None], score, NEG_INF)
        _, idx = lax.top_k(score, n_sel)
        sel = jax.vmap(lambda kv_b, i_b: kv_b[i_b])(kv, idx)
        ok = (k_pos[idx] // CHUNK) <= (qp[None, :, None] // CHUNK)
        s = jnp.einsum('bqhd,bqnhd->bhqn', qq, sel[:, :, :, 0]).astype(jnp.float32) * scale
        s = jnp.where(ok[:, None], s, NEG_INF)
        p = jax.nn.softmax(s, axis=-1).astype(kv.dtype)
        return jnp.einsum('bhqn,bqnhd->bqhd', p, sel[:, :, :, 1])

    return map_query_blocks(blk, q_pos, q, q_idx, w_idx)


def trunk_layer(x, pos, past, w_in, mla_kv_norm, mla_w_uk, mla_w_uv, band_rel_bias, w_branch,
                w_out, ln1_g, ln1_b, w_gate_up, w_down, ln2_g, ln2_b):
    bsz, s_len, _ = x.shape
    (a_q, a_ckv, a_kr, sb_qkv, bd_qkv, ds_qkv, ix_q, ix_k, ix_w, gate_logits) = split_cols(x @ w_in)

    a_q = a_q.reshape(bsz, s_len, N_HEADS, MLA_D_NOPE + MLA_D_ROPE)
    q_nope = a_q[..., :MLA_D_NOPE]
    q_rope = rope(a_q[..., MLA_D_NOPE:], pos, MLA_THETA)
    k_rope_new = rope(a_kr[:, :, None, :], pos, MLA_THETA)[:, :, 0]
    lat_new = jnp.concatenate([rms_norm(a_ckv, mla_kv_norm), k_rope_new], axis=-1)

    sb = sb_qkv.reshape(bsz, s_len, 3, N_HEADS, HEAD_DIM)
    sb_q, sb_kv_new = sb[:, :, 0], sb[:, :, 1:]

    bd = bd_qkv.reshape(bsz, s_len, 3, N_HEADS, HEAD_DIM)
    bd_q, bd_kv_new = bd[:, :, 0], bd[:, :, 1:]

    ds = ds_qkv.reshape(bsz, s_len, 3, N_HEADS, HEAD_DIM)
    ds_q = partial_rope(ds[:, :, 0], pos)
    ds_kv_new = jnp.stack([partial_rope(ds[:, :, 1], pos), ds[:, :, 2]], axis=2)
    ix_q = partial_rope(ix_q.reshape(bsz, s_len, IDX_HEADS, IDX_DIM), pos)
    kidx_new = partial_rope(ix_k[:, :, None, :], pos)[:, :, 0]

    if past is None:
        k_pos = pos
        lat_all, sb_kv_all, ds_kv_all, kidx_all = lat_new, sb_kv_new, ds_kv_new, kidx_new
        o_c = band_prompt(bd_q, bd_kv_new, band_rel_bias)
        band_rows = bd_kv_new[:, s_len - min(BAND_WINDOW, s_len):]
    else:
        past_lat, past_sb, past_band, past_ds, past_kidx = past
        p_len = past_lat.shape[1]
        w_len = past_band.shape[1]
        k_pos = jnp.concatenate([jnp.arange(p_len, dtype=jnp.int32), pos])
        lat_all = jnp.concatenate([past_lat, lat_new], axis=1)
        sb_kv_all = jnp.concatenate([past_sb, sb_kv_new], axis=1)
        ds_kv_all = jnp.concatenate([past_ds, ds_kv_new], axis=1)
        kidx_all = jnp.concatenate([past_kidx, kidx_new], axis=1)
        band_kv = jnp.concatenate([past_band, bd_kv_new], axis=1)
        band_pos = jnp.concatenate([jnp.arange(p_len - w_len, p_len, dtype=jnp.int32), pos])
        o_c = band_attention(bd_q[:, None], band_kv[:, None, :, 0], band_kv[:, None, :, 1],
                             pos[None], band_pos[None], band_rel_bias)[:, 0]
        band_rows = bd_kv_new

    ckv_all, kr_all = lat_all[..., :MLA_D_C], lat_all[..., MLA_D_C:]
    k_nope = jnp.einsum('bkc,chd->bkhd', ckv_all, mla_w_uk)
    v_a = jnp.einsum('bkc,chd->bkhd', ckv_all, mla_w_uv)
    o_a = mla_attention(q_nope, q_rope, k_nope, kr_all, v_a, pos, k_pos)
    o_b = stick_breaking_attention(sb_q, sb_kv_all[:, :, 0], sb_kv_all[:, :, 1], pos, k_pos)
    o_d = dsa_attention(ds_q, ix_q, ix_w, ds_kv_all, kidx_all, pos, k_pos)

    branches = jnp.stack([o_a, o_b, o_c, o_d], axis=2).reshape(bsz, s_len, N_BRANCH, BRANCH_WIDTH)
    gates = jax.nn.sigmoid(gate_logits.reshape(bsz, s_len, N_BRANCH, D_MODEL))
    merged = jnp.einsum('bsnd,bsnd->bsd', gates, jnp.einsum('bsnv,nvd->bsnd', branches, w_branch))
    x = layer_norm(ALPHA * x + merged @ w_out, ln1_g, ln1_b)

    gu = x @ w_gate_up
    ffn = (jax.nn.silu(gu[..., :D_FF]) * gu[..., D_FF:]) @ w_down
    x = layer_norm(ALPHA * x + ffn, ln2_g, ln2_b)
    return x, (lat_new, sb_kv_new, band_rows, ds_kv_new, kidx_new)


def setup_inputs(seed: int = 0) -> dict:
    key = jax.random.key(seed)
    ks = jax.random.split(key, 20)
    f32 = jnp.float32

    def nrm(k, shape, scale):
        return jax.random.normal(k, shape, f32) * scale

    band_rows = min(BAND_WINDOW, PAST_LEN)
    return {
        'x_prompt': nrm(ks[0], (BATCH, SEQ, D_MODEL), 1.0),
        'x_sample': nrm(ks[1], (DEC_BATCH, DEC_SEQ, D_MODEL), 1.0),
        'cache_mla_latent': nrm(ks[2], (DEPTH, DEC_BATCH, PAST_LEN, MLA_D_C + MLA_D_ROPE), 1.0),
        'cache_sb_kv': nrm(ks[3], (DEPTH, DEC_BATCH, PAST_LEN, 2, N_HEADS, HEAD_DIM), 1.0),
        'cache_band_kv': nrm(ks[4], (DEPTH, DEC_BATCH, band_rows, 2, N_HEADS, HEAD_DIM), 1.0),
        'cache_dsa_kv': nrm(ks[5], (DEPTH, DEC_BATCH, PAST_LEN, 2, N_HEADS, HEAD_DIM), 1.0),
        'cache_dsa_kidx': nrm(ks[6], (DEPTH, DEC_BATCH, PAST_LEN, IDX_DIM), 1.0),
        'w_in': nrm(ks[7], (DEPTH, D_MODEL, IN_WIDTH), D_MODEL ** -0.5),
        'mla_kv_norm': 1.0 + nrm(ks[8], (DEPTH, MLA_D_C), 0.01),
        'mla_w_uk': nrm(ks[9], (DEPTH, MLA_D_C, N_HEADS, MLA_D_NOPE), MLA_D_C ** -0.5),
        'mla_w_uv': nrm(ks[10], (DEPTH, MLA_D_C, N_HEADS, MLA_D_V), MLA_D_C ** -0.5),
        'band_rel_bias': nrm(ks[11], (DEPTH, N_HEADS, 2 * REL_CLIP + 1), 0.1),
        'w_branch': nrm(ks[12], (DEPTH, N_BRANCH, BRANCH_WIDTH, D_MODEL), BRANCH_WIDTH ** -0.5),
        'w_out': nrm(ks[13], (DEPTH, D_MODEL, D_MODEL), BETA * D_MODEL ** -0.5),
        'ln1_g': 1.0 + nrm(ks[14], (DEPTH, D_MODEL), 0.01),
        'ln1_b': nrm(ks[15], (DEPTH, D_MODEL), 0.01),
        'w_gate_up': nrm(ks[16], (DEPTH, D_MODEL, 2 * D_FF), D_MODEL ** -0.5),
        'w_down': nrm(ks[17], (DEPTH, D_FF, D_MODEL), BETA * D_FF ** -0.5),
        'ln2_g': 1.0 + nrm(ks[18], (DEPTH, D_MODEL), 0.01),
        'ln2_b': nrm(ks[19], (DEPTH, D_MODEL), 0.01),
    }


def reference(x_prompt, x_sample, cache_mla_latent, cache_sb_kv, cache_band_kv, cache_dsa_kv,
              cache_dsa_kidx, w_in, mla_kv_norm, mla_w_uk, mla_w_uv, band_rel_bias, w_branch,
              w_out, ln1_g, ln1_b, w_gate_up, w_down, ln2_g, ln2_b):
    past_len = cache_mla_latent.shape[2]
    pos_p = jnp.arange(x_prompt.shape[1], dtype=jnp.int32)
    pos_s = past_len + jnp.arange(x_sample.shape[1], dtype=jnp.int32)
    xp, xs = x_prompt, x_sample
    st_p, st_s = [], []
    for l in range(DEPTH):
        params = (w_in[l], mla_kv_norm[l], mla_w_uk[l], mla_w_uv[l], band_rel_bias[l], w_branch[l],
                  w_out[l], ln1_g[l], ln1_b[l], w_gate_up[l], w_down[l], ln2_g[l], ln2_b[l])
        xp, new_p = trunk_layer(xp, pos_p, None, *params)
        past = (cache_mla_latent[l], cache_sb_kv[l], cache_band_kv[l], cache_dsa_kv[l], cache_dsa_kidx[l])
        xs, new_s = trunk_layer(xs, pos_s, past, *params)
        st_p.append(new_p)
        st_s.append(new_s)
    lat_p = jnp.stack([s[0] for s in st_p])
    sb_p = jnp.stack([s[1] for s in st_p])
    band_p = jnp.stack([s[2] for s in st_p])
    dsa_p = jnp.stack([s[3] for s in st_p])
    kidx_p = jnp.stack([s[4] for s in st_p])
    lat_s = jnp.stack([s[0] for s in st_s])
    sb_s = jnp.stack([s[1] for s in st_s])
    band_s = jnp.stack([s[2] for s in st_s])
    dsa_s = jnp.stack([s[3] for s in st_s])
    kidx_s = jnp.stack([s[4] for s in st_s])
    return (xp, xs, lat_p, sb_p, band_p, dsa_p, kidx_p, lat_s, sb_s, band_s, dsa_s, kidx_s)
```

```python
import math
import numpy as np
import ml_dtypes
import concourse.bass as bass
import concourse.mybir as mybir
from concourse.bass_utils import run_bass_kernel_spmd

F32 = mybir.dt.float32
BF16 = mybir.dt.bfloat16
AF = mybir.ActivationFunctionType
ALU = mybir.AluOpType
AX = mybir.AxisListType

NCORES = 8
D = 1024
DEPTH = 2
NT = 17
PAST = 4096
NEG = -30000.0
TOPK = 256
N_BISECT = 18
D_FF = 2816
ALPHA = (2 * DEPTH) ** 0.25
LN_EPS = 1e-5
NPROJ = 3432
C_AQ, C_CKV, C_KR, C_SB, C_BD, C_DS, C_IXQ, C_IXK, C_IXW, C_GATE = 0, 384, 512, 544, 1312, 2080, 2848, 3360, 3424, 3432
VW = 320
O_KTM, O_VM, O_KTS, O_VS, O_KTB, O_VB, O_KTD, O_VD, O_KTI = 0, 49152, 90112, 122880, 163840, 196608, 237568, 270336, 311296
SLOTSZ = 319488
TOT = 16 * SLOTSZ
Q_M, Q_S, Q_B, Q_D, Q_I = 0, 49152, 81920, 114688, 147456
QSZ = 212992
NSB = 33


class Buf:
    __slots__ = ("w", "r")

    def __init__(self):
        self.w = None
        self.r = {}


class Tile:
    def __init__(self, h):
        self.h = h
        self.b = Buf()


class FW:
    ENG = ("pe", "act", "dve", "pool", "sp")

    def __init__(self, nc):
        self.nc = nc
        self.ops = {e: [] for e in self.ENG}
        self.sem = {e: nc.alloc_semaphore("prog_" + e) for e in self.ENG}
        self.cnt = {e: 0 for e in self.ENG}
        self.seen = {e: {} for e in self.ENG}
        self.needed = {e: set() for e in self.ENG}
        self.semobj = {}
        for e in self.ENG:
            self.semobj[("c", e)] = self.sem[e]
        self.dq = {}
        for q, n in (("sp", 20), ("pool", 12), ("act", 6)):
            sems = [nc.alloc_semaphore("dq_%s_%d" % (q, i)) for i in range(n)]
            for i, s in enumerate(sems):
                self.semobj[("d", q, i)] = s
            self.dq[q] = {"n": n, "rr": 0, "val": [0] * n, "last": [None] * n}
        self.ccsem = nc.alloc_semaphore("ccsem")
        self.semobj[("cc",)] = self.ccsem
        self.ccval = 0

    def _wait(self, eng, tok):
        key, val, teng = tok
        if teng == eng and eng == "pe":
            return
        if self.seen[eng].get(key, 0) >= val:
            return
        self.seen[eng][key] = val
        if key[0] == "c":
            self.needed[key[1]].add(val)
            self.ops[eng].append(("w", key, val))
        else:
            sem = self.semobj[key]
            self.ops[eng].append(lambda e: e.wait_ge(sem, val))

    def _deps(self, reads, writes):
        deps = []
        for b in reads:
            if b.w is not None:
                deps.append(b.w)
        for b in writes:
            if b.w is not None:
                deps.append(b.w)
            deps.extend(b.r.values())
        return deps

    def _commit(self, tok, rkey, reads, writes):
        for b in reads:
            b.r[rkey] = tok
        for b in writes:
            b.w = tok
            b.r = {}

    def op(self, eng, fn, reads, writes):
        reads = [x.b if isinstance(x, Tile) else x for x in reads]
        writes = [x.b if isinstance(x, Tile) else x for x in writes]
        for tok in self._deps(reads, writes):
            self._wait(eng, tok)
        self.cnt[eng] += 1
        val = self.cnt[eng]
        self.ops[eng].append(("o", fn, val))
        tok = (("c", eng), val, eng)
        self._commit(tok, ("c", eng), reads, writes)
        return tok

    def dma(self, q, out, in_, reads, writes):
        reads = [x.b if isinstance(x, Tile) else x for x in reads]
        writes = [x.b if isinstance(x, Tile) else x for x in writes]
        st = self.dq[q]
        k = st["rr"]
        st["rr"] = (k + 1) % st["n"]
        if st["last"][k] is not None:
            self._wait(q, st["last"][k])
        for tok in self._deps(reads, writes):
            self._wait(q, tok)
        st["val"][k] += 16
        val = st["val"][k]
        key = ("d", q, k)
        sem = self.semobj[key]
        self.ops[q].append(lambda e: e.dma_start(out=out, in_=in_).then_inc(sem, 16))
        tok = (key, val, None)
        st["last"][k] = tok
        self._commit(tok, key, reads, writes)
        return tok

    def allgather(self, in_ap, out_ap, reads, writes):
        q = "pool"
        for tok in self._deps(reads, writes):
            self._wait(q, tok)
        self.ccval += 1
        val = self.ccval
        sem = self.ccsem
        self.ops[q].append(lambda e: e.collective_compute(
            "AllGather", ALU.bypass, replica_groups=[list(range(NCORES))],
            ins=[in_ap.opt()], outs=[out_ap.opt()]).then_inc(sem, 1))
        tok = (("cc",), val, None)
        self._commit(tok, ("cc",), reads, writes)

    def all_tokens(self):
        toks = []
        for e in self.ENG:
            if self.cnt[e] > 0:
                toks.append((("c", e), self.cnt[e], e))
        for q, st in self.dq.items():
            for t in st["last"]:
                if t is not None:
                    toks.append(t)
        if self.ccval:
            toks.append((("cc",), self.ccval, None))
        return toks

    def barrier(self, engines=None):
        toks = self.all_tokens()
        for e in (engines or self.ENG):
            for t in toks:
                if t[2] == e:
                    continue
                self._wait(e, t)

    def emit(self):
        nc = self.nc
        self.barrier(["sp"])
        rank = {}
        for en in self.ENG:
            rank[en] = {v: i + 1 for i, v in enumerate(sorted(self.needed[en]))}
        sems = self.sem

        def run(en, e):
            rk = rank[en]
            sem = sems[en]
            for f in self.ops[en]:
                if isinstance(f, tuple):
                    if f[0] == "o":
                        ins = f[1](e)
                        if f[2] in rk:
                            ins.then_inc(sem, 1)
                    else:
                        e.wait_ge(sems[f[1][1]], rank[f[1][1]][f[2]])
                else:
                    f(e)

        with nc.Block() as block:
            @block.tensor
            def _(e):
                run("pe", e)

            @block.scalar
            def _(e):
                run("act", e)

            @block.vector
            def _(e):
                run("dve", e)

            @block.gpsimd
            def _(e):
                run("pool", e)

            @block.sync
            def _(e):
                run("sp", e)


def _dtsize(dt):
    return 4 if dt == F32 else 2


class Arena:
    def __init__(self, nc, base, size):
        self.nc, self.base, self.size, self.off, self.n = nc, base, size, 0, 0

    def mark(self):
        return self.off

    def reset(self, m):
        self.off = m

    def __call__(self, shape, dt):
        nb = int(np.prod(shape[1:])) * _dtsize(dt)
        nb = (nb + 63) // 64 * 64
        assert self.off + nb <= self.size, ("SBUF arena overflow", self.off, nb, self.size)
        h = self.nc.alloc_sbuf_tensor_at("t%d" % self.n, list(shape), dt, offset=self.base + self.off)
        self.off += nb
        self.n += 1
        return Tile(h)


class Ring:
    def __init__(self, tiles):
        self.t, self.i = tiles, 0

    def next(self):
        x = self.t[self.i]
        self.i = (self.i + 1) % len(self.t)
        return x


def build_program(dbg=None):
    dbg = dbg or {}
    NL = dbg.get('layers', DEPTH)
    PH = dbg.get('phases', ('p1', 'ag', 'p3', 'p4a', 'p4b'))
    TILES = dbg.get('tiles', list(range(NT)))
    PAST_TILES = dbg.get('past_tiles', 32)
    MIX = dbg.get('mix', 'iabcd')
    nc = bass.Bass("TRN2", target_bir_lowering=False)
    fw = FW(nc)

    def din(name, shape, dt=F32):
        return nc.dram_tensor(name, list(shape), dt, kind="ExternalInput")

    def dout(name, shape, dt=F32):
        return nc.dram_tensor(name, list(shape), dt, kind="ExternalOutput")

    def dscr(name, shape, dt):
        return nc.dram_tensor(name, list(shape), dt)

    def AP(t, off, ap):
        return bass.AP(tensor=t, offset=off, ap=[list(x) for x in ap])

    x_in = din("x_in", [NT, 128, D])
    c_lat = din("c_lat", [DEPTH, PAST, 160])
    c_sb = din("c_sb", [DEPTH, PAST, 512])
    c_bd = din("c_bd", [DEPTH, 512, 512])
    c_ds = din("c_ds", [DEPTH, PAST, 512])
    c_ix = din("c_ix", [DEPTH, PAST, 64])
    w_in = din("w_in", [DEPTH, D, 7528])
    kvn = din("mla_kv_norm", [DEPTH, 128])
    w_uk = din("mla_w_uk", [DEPTH, 128, 256])
    w_uv = din("mla_w_uv", [DEPTH, 128, 256])
    w_br = din("w_branch", [DEPTH, 1024, D])
    w_out = din("w_out", [DEPTH, D, D])
    ln1g = din("ln1_g", [DEPTH, D]); ln1b = din("ln1_b", [DEPTH, D])
    w_gu = din("w_gate_up", [DEPTH, D, 2 * D_FF])
    w_dn = din("w_down", [DEPTH, D_FF, D])
    ln2g = din("ln2_g", [DEPTH, D]); ln2b = din("ln2_b", [DEPTH, D])
    ropeM = din("ropeM", [NT, 128, 2, 64])
    ropeP = din("ropeP", [NT, 128, 2, 72])
    cmat = din("cmat", [128, 3, 128], BF16)
    m_cc = din("m_cc", [128, 8, 128], BF16)
    m_sb = din("m_sb", [128, 8, 128], BF16)
    ms_cc = din("ms_cc", [128, 1, 128], BF16)
    ms_sb = din("ms_sb", [128, 1, 128], BF16)
    dsa_madd = din("dsa_madd", [128, 1024])
    dsas_madd = din("dsas_madd", [128, 128])
    bd_bias = din("bd_bias", [DEPTH, 128, 16 * 512], BF16)
    bds_bias = din("bds_bias", [DEPTH, 128, 5 * 512], BF16)

    y_out = dout("y", [NT, 128, D])
    o_lat = dout("o_lat", [DEPTH, NT, 128, 160])
    o_sb = dout("o_sb", [DEPTH, NT, 128, 512])
    o_bd = dout("o_bd", [DEPTH, 2, 128, 512])
    o_ds = dout("o_ds", [DEPTH, NT, 128, 512])
    o_ix = dout("o_ix", [DEPTH, NT, 128, 64])

    agin = [dscr("agin%d" % l, [128, TOT // 128], BF16) for l in range(DEPTH)]
    agout = [dscr("agout%d" % l, [1024, TOT // 128], BF16) for l in range(DEPTH)]
    agin_b = [[Buf() for _ in range(16)] for _ in range(DEPTH)]
    agout_b = [Buf() for _ in range(DEPTH)]
    s_kt = {}
    s_v = {}
    for l in range(DEPTH):
        s_kt[l] = {"m": dscr("skt_m%d" % l, [4 * 96 * NSB * 128], BF16), "s": dscr("skt_s%d" % l, [4 * 64 * NSB * 128], BF16),
                   "d": dscr("skt_d%d" % l, [4 * 64 * NSB * 128], BF16), "b": dscr("skt_b%d" % l, [4 * 64 * 5 * 128], BF16),
                   "i": dscr("skt_i%d" % l, [64 * NSB * 128], BF16)}
        s_v[l] = {"m": dscr("sv_m%d" % l, [NSB * 128 * VW], BF16), "s": dscr("sv_s%d" % l, [NSB * 128 * VW], BF16),
                  "d": dscr("sv_d%d" % l, [NSB * 128 * VW], BF16), "b": dscr("sv_b%d" % l, [5 * 128 * VW], BF16)}
    skv_b = [Buf() for _ in range(DEPTH)]
    qscr = dscr("qscr", [NT * QSZ], BF16)
    qscr_b = [Buf() for _ in range(NT)]
    xtscr = [dscr("xtscr%d" % i, [NT, 128, 1024], BF16) for i in range(2)]
    xtscr_b = [[Buf() for _ in range(NT)] for _ in range(2)]
    xres = [dscr("xres%d" % i, [NT, 128, D], F32) for i in range(2)]
    xres_b = [[Buf() for _ in range(NT)] for _ in range(2)]

    ARENA_BASE, ARENA_SIZE = 16512, 212736
    _placeholder = nc.alloc_sbuf_tensor("arena", [128, ARENA_SIZE // 4], F32)
    A = Arena(nc, ARENA_BASE, ARENA_SIZE)
    pf = [Tile(nc.alloc_psum_tensor("pf%d" % i, [128, 512], F32)) for i in range(6)]
    pb = [Tile(nc.alloc_psum_tensor("pb%d" % i, [128, 1024], BF16)) for i in range(2)]

    cm = A([128, 3, 128], BF16)
    ixw = A([128, NT, 8], F32)
    fw.dma("sp", cm.h[:], cmat.ap(), [], [cm])
    ident = cm.h[:, 0, :]
    tri = cm.h[:, 1, :]
    lo_m = cm.h[:, 2, :]
    mark0 = A.mark()
    for _bk in (pf + pb if dbg.get('sim') else []):
        fw.op('dve', lambda e, _bk=_bk: e.memset(_bk.h[:], 0.0), [], [_bk])

    def mm(out, lhsT, rhs, start, stop, R, W, skip=False):
        if skip:
            fw.op("pe", lambda e: e.matmul(out, lhsT, rhs, start=start, stop=stop, skip_group_check=True), R, W)
        else:
            fw.op("pe", lambda e: e.matmul(out, lhsT, rhs, start=start, stop=stop), R, W)

    def tr(out, in_, R, W):
        n = in_.shape[0]
        idn = cm.h[0:n, 0, 0:n]
        fw.op("pe", lambda e: e.transpose(out, in_, idn), list(R) + [cm], W)

    def act(out, in_, func, R, W, bias=None, scale=None, accum=None):
        kw = {}
        if bias is not None:
            kw["bias"] = bias
        if scale is not None:
            kw["scale"] = scale
        if accum is not None:
            kw["accum_out"] = accum
        fw.op("act", lambda e: e.activation(out=out, in_=in_, func=func, **kw), R, W)

    def ts(eng, out, in0, s1, s2, op0, op1, R, W, accum=None):
        kw = {}
        if op1 is not None:
            kw["op1"] = op1
        if accum is not None:
            kw["accum_out"] = accum
        fw.op(eng, lambda e: e.tensor_scalar(out=out, in0=in0, scalar1=s1, scalar2=s2, op0=op0, **kw), R, W)

    def tt(eng, out, in0, in1, op, R, W):
        fw.op(eng, lambda e: e.tensor_tensor(out=out, in0=in0, in1=in1, op=op), R, W)

    def stt(eng, out, in0, scalar, in1, op0, op1, R, W):
        fw.op(eng, lambda e: e.scalar_tensor_tensor(out=out, in0=in0, scalar=scalar, in1=in1, op0=op0, op1=op1), R, W)

    def cp(eng, out, in_, R, W):
        if eng == "act":
            fw.op("act", lambda e: e.copy(out=out, in_=in_), R, W)
        else:
            fw.op(eng, lambda e: e.tensor_copy(out=out, in_=in_), R, W)

    def memset(eng, ap, val, W):
        fw.op(eng, lambda e: e.memset(ap, val), [], W)

    cast_rr = [0]

    def cast_eng():
        cast_rr[0] += 1
        return ("dve", "pool")[cast_rr[0] % 2]

    def bcast_rows(t, off, n):
        return AP(t, off, [[0, 128], [1, n]])

    def load_w(dst, src_t, src_off, row_stride, nchunk, ncols, stg, col0=0):
        for c in range(nchunk):
            sw = int(stg.t[0].h.shape[1])
            for j0 in range(0, ncols, sw):
                n = min(sw, ncols - j0)
                s = stg.next()
                fw.dma("sp", s.h[:, 0:n], AP(src_t, src_off + c * 128 * row_stride + col0 + j0, [[row_stride, 128], [1, n]]), [], [s])
                cp(cast_eng(), dst.h[:, c, j0:j0 + n], s.h[:, 0:n], [s], [dst])

    def layer_norm(y, g_t, b_t, out_t, tmpc, junk, st):
        fw.op("dve", lambda e: e.reduce_sum(out=st.h[:, 0:1], in_=y.h[:], axis=AX.X), [y], [st])
        ts("dve", st.h[:, 1:2], st.h[:, 0:1], 1.0 / D, None, ALU.mult, None, [st], [st])
        ts("dve", tmpc.h[:], y.h[:], st.h[:, 1:2], None, ALU.subtract, None, [y, st], [tmpc])
        memset("dve", st.h[:, 2:3], 0.0, [st])
        act(junk.h[:], tmpc.h[:], AF.Square, [tmpc, st], [junk, st], accum=st.h[:, 2:3])
        ts("dve", st.h[:, 3:4], st.h[:, 2:3], 1.0 / D, LN_EPS, ALU.mult, ALU.add, [st], [st])
        act(st.h[:, 4:5], st.h[:, 3:4], AF.Sqrt, [st], [st])
        fw.op("dve", lambda e: e.reciprocal(out=st.h[:, 5:6], in_=st.h[:, 4:5]), [st], [st])
        stt("dve", tmpc.h[:], tmpc.h[:], st.h[:, 5:6], g_t.h[:], ALU.mult, ALU.mult, [tmpc, st, g_t], [tmpc])
        tt("dve", out_t.h[:], tmpc.h[:], b_t.h[:], ALU.add, [tmpc, b_t], [out_t])

    KROWS = {"m": 96, "s": 64, "b": 64, "d": 64}
    KT_OFF = {"m": O_KTM, "s": O_KTS, "b": O_KTB, "d": O_KTD}
    V_OFF = {"m": O_VM, "s": O_VS, "b": O_VB, "d": O_VD}

    def ag_kt_src(l, kind, slot):
        r = KROWS[kind]
        return AP(agout[l], slot * SLOTSZ + KT_OFF[kind], [[512, r], [TOT, 8], [1, 512]])

    def ag_v_src(l, kind, slot):
        return AP(agout[l], slot * SLOTSZ + V_OFF[kind], [[VW, 128], [TOT, 8], [1, VW]])

    def ag_kti_src(l, slot):
        return AP(agout[l], slot * SLOTSZ + O_KTI, [[128, 64], [TOT, 8], [1, 128]])

    def s_kt_src(l, kind, blk0, nb):
        r = KROWS[kind]
        nblk = 5 if kind == "b" else NSB
        return AP(s_kt[l][kind], blk0 * 512, [[nblk * 512, r], [1, nb * 512]])

    def s_v_src(l, kind, blk0, nb):
        return AP(s_v[l][kind], blk0 * 128 * VW, [[VW, 128], [128 * VW, nb], [1, VW]])

    def s_kti_src(l, blk0, nb):
        return AP(s_kt[l]["i"], blk0 * 128, [[NSB * 128, 64], [128, nb], [1, 128]])

    for l in range(NL):
        xsrc_t, xsrc_b = (x_in, None) if l == 0 else (xres[1], xres_b[1])
        fw.barrier()
        A.reset(mark0)
        W = A([128, 8, NPROJ], BF16)
        wuk = A([128, 256], BF16)
        wuv = A([128, 256], BF16)
        gk = A([128, 128], F32)
        stg = Ring([A([128, 2048], F32) for _ in range(2)])
        fw.dma("sp", gk.h[:], bcast_rows(kvn, l * 128, 128), [], [gk])
        for (dst, src) in ((wuk, w_uk), (wuv, w_uv)):
            s = stg.next()
            fw.dma("sp", s.h[:, 0:256], AP(src, l * 128 * 256, [[256, 128], [1, 256]]), [], [s])
            cp("dve", dst.h[:], s.h[:, 0:256], [s], [dst])
        load_w(W, w_in, l * D * 7528, 7528, 8, NPROJ, stg)

        xt_r = Ring([A([128, D], F32) for _ in range(2)])
        xb_r = Ring([A([128, D], BF16) for _ in range(2)])
        xT_r = Ring([A([128, D], BF16) for _ in range(2)])
        P_r = Ring([A([128, NPROJ], F32) for _ in range(2)])
        Pb_r = Ring([A([128, NPROJ], BF16) for _ in range(2)])
        rM_r = Ring([A([128, 2, 64], F32) for _ in range(2)])
        rP_r = Ring([A([128, 2, 72], F32) for _ in range(2)])
        rt = [A([128, 72], F32) for _ in range(4)]
        sm = A([128, 8], F32)
        junk = A([128, 128], F32)
        stA_r = Ring([A([128, 1024], BF16) for _ in range(3)])
        ckvT_r = Ring([A([128, 128], BF16) for _ in range(2)])
        ktm_r = Ring([A([96, 4, 128], BF16) for _ in range(2)])
        vst_r = Ring([A([128, 4, VW], BF16) for _ in range(2)])
        for vt in vst_r.t:
            memset("pool", vt.h[:], 1.0, [vt])
        pfr = Ring(pf[0:4])
        pbr = Ring(pb)
        evac_rr = [0]

        def evac_eng():
            evac_rr[0] += 1
            return ("act", "dve")[evac_rr[0] % 2]

        def rope(Pt, col0, nh, hstride, half, tab, tcol):
            v = Pt.h[:, col0:col0 + nh * hstride].rearrange("p (h d) -> p h d", h=nh)
            x1 = v[:, :, 0:half]
            x2 = v[:, :, half:2 * half]
            cos = tab.h[:, 0, tcol:tcol + nh * half].rearrange("p (h d) -> p h d", h=nh)
            sin = tab.h[:, 1, tcol:tcol + nh * half].rearrange("p (h d) -> p h d", h=nh)
            t = [x.h[:, 0:nh * half].rearrange("p (h d) -> p h d", h=nh) for x in rt]
            tt("dve", t[0], x1, cos, ALU.mult, [Pt, tab], [rt[0]])
            tt("dve", t[1], x2, sin, ALU.mult, [Pt, tab], [rt[1]])
            tt("dve", t[2], x2, cos, ALU.mult, [Pt, tab], [rt[2]])
            tt("dve", t[3], x1, sin, ALU.mult, [Pt, tab], [rt[3]])
            tt("dve", x1, t[0], t[1], ALU.subtract, [rt[0], rt[1]], [Pt])
            tt("dve", x2, t[2], t[3], ALU.add, [rt[2], rt[3]], [Pt])

        def emit_kv(Pb, dest, with_bd):
            def kt_dst(kind):
                r = KROWS[kind]
                if dest[0] == "ag":
                    return AP(agin[l], dest[1] * SLOTSZ + KT_OFF[kind], [[512, r], [128, 4], [1, 128]])
                blk = dest[2] if kind == "b" else dest[1]
                nblk = 5 if kind == "b" else NSB
                return AP(s_kt[l][kind], blk * 512, [[nblk * 512, r], [128, 4], [1, 128]])

            def v_dst(kind):
                if dest[0] == "ag":
                    return AP(agin[l], dest[1] * SLOTSZ + V_OFF[kind], [[VW, 128], [1, VW]])
                blk = dest[2] if kind == "b" else dest[1]
                return AP(s_v[l][kind], blk * 128 * VW, [[VW, 128], [1, VW]])

            if dest[0] == "ag":
                kti_dst = AP(agin[l], dest[1] * SLOTSZ + O_KTI, [[128, 64], [1, 128]])
                dbuf = agin_b[l][dest[1]]
            else:
                kti_dst = AP(s_kt[l]["i"], dest[1] * 128, [[NSB * 128, 64], [1, 128]])
                dbuf = skv_b[l]
            bk = pbr.next()
            tr(bk.h[:, 0:128], Pb.h[:, C_CKV:C_CKV + 128], [Pb], [bk])
            tr(bk.h[0:96, 128:256], Pb.h[:, C_KR - 64:C_KR + 32], [Pb], [bk])
            for h in range(4):
                tr(bk.h[0:64, (2 + h) * 128:(3 + h) * 128], Pb.h[:, C_SB + 256 + h * 64:C_SB + 256 + (h + 1) * 64], [Pb], [bk])
            tr(bk.h[0:64, 768:896], Pb.h[:, C_IXK:C_IXK + 64], [Pb], [bk])
            sa = stA_r.next()
            cp(evac_eng(), sa.h[:, 0:896], bk.h[:, 0:896], [bk], [sa])
            fw.dma("pool", kt_dst("s"), sa.h[0:64, 256:768].rearrange("p (h k) -> p h k", h=4), [sa], [dbuf])
            fw.dma("pool", kti_dst, sa.h[0:64, 768:896], [sa], [dbuf])
            ckvT = ckvT_r.next()
            cp("pool", ckvT.h[:], sa.h[:, 0:128], [sa], [ckvT])
            ktm = ktm_r.next()
            pk = pf[4]
            for h in range(4):
                mm(pk.h[0:64, h * 128:(h + 1) * 128], wuk.h[:, h * 64:(h + 1) * 64], ckvT.h[:], True, True, [wuk, ckvT], [pk])
            cp(evac_eng(), ktm.h[0:64, :, :], pk.h[0:64, :].rearrange("p (h k) -> p h k", h=4), [pk], [ktm])
            for h in range(4):
                cp("pool", ktm.h[64:96, h, :], sa.h[64:96, 128:256], [sa], [ktm])
            fw.dma("pool", kt_dst("m"), ktm.h[:], [ktm], [dbuf])
            pv = pf[5]
            mm(pv.h[:, 0:256], ckvT.h[:], wuv.h[:], True, True, [ckvT, wuv], [pv])
            vt = vst_r.next()
            cp(evac_eng(), vt.h[:, 0, :].rearrange("p (h c) -> p h c", h=4)[:, :, 0:64],
               pv.h[:, 0:256].rearrange("p (h c) -> p h c", h=4), [pv], [vt])
            fw.dma("pool", v_dst("m"), vt.h[:, 0, :], [vt], [dbuf])
            for j, (kind, c0) in enumerate((("s", C_SB + 512), ("b", C_BD + 512), ("d", C_DS + 512))):
                if kind == "b" and not with_bd:
                    continue
                cp("pool", vt.h[:, 1 + j, :].rearrange("p (h c) -> p h c", h=4)[:, :, 0:64],
                   Pb.h[:, c0:c0 + 256].rearrange("p (h c) -> p h c", h=4), [Pb], [vt])
                fw.dma("pool", v_dst(kind), vt.h[:, 1 + j, :], [vt], [dbuf])
            bk = pbr.next()
            for h in range(4):
                tr(bk.h[0:64, h * 128:(h + 1) * 128], Pb.h[:, C_BD + 256 + h * 64:C_BD + 256 + (h + 1) * 64], [Pb], [bk])
                tr(bk.h[0:64, (4 + h) * 128:(5 + h) * 128], Pb.h[:, C_DS + 256 + h * 64:C_DS + 256 + (h + 1) * 64], [Pb], [bk])
            sa = stA_r.next()
            cp(evac_eng(), sa.h[0:64, :], bk.h[0:64, :], [bk], [sa])
            if with_bd:
                fw.dma("pool", kt_dst("b"), sa.h[0:64, 0:512].rearrange("p (h k) -> p h k", h=4), [sa], [dbuf])
            fw.dma("pool", kt_dst("d"), sa.h[0:64, 512:1024].rearrange("p (h k) -> p h k", h=4), [sa], [dbuf])

        def emit_q(Pb, t):
            qb_ = qscr_b[t]
            bk = pbr.next()
            for h in range(4):
                tr(bk.h[0:96, h * 128:(h + 1) * 128], Pb.h[:, C_AQ + h * 96:C_AQ + (h + 1) * 96], [Pb], [bk])
                tr(bk.h[0:64, (4 + h) * 128:(5 + h) * 128], Pb.h[:, C_SB + h * 64:C_SB + (h + 1) * 64], [Pb], [bk])
            sa = stA_r.next()
            cp(evac_eng(), sa.h[0:96, :], bk.h[0:96, :], [bk], [sa])
            fw.dma("pool", AP(qscr, t * QSZ + Q_M, [[512, 96], [1, 512]]), sa.h[0:96, 0:512], [sa], [qb_])
            fw.dma("pool", AP(qscr, t * QSZ + Q_S, [[512, 64], [1, 512]]), sa.h[0:64, 512:1024], [sa], [qb_])
            bk = pbr.next()
            for h in range(4):
                tr(bk.h[0:64, h * 128:(h + 1) * 128], Pb.h[:, C_BD + h * 64:C_BD + (h + 1) * 64], [Pb], [bk])
                tr(bk.h[0:64, (4 + h) * 128:(5 + h) * 128], Pb.h[:, C_DS + h * 64:C_DS + (h + 1) * 64], [Pb], [bk])
            sa = stA_r.next()
            cp(evac_eng(), sa.h[0:64, :], bk.h[0:64, :], [bk], [sa])
            fw.dma("pool", AP(qscr, t * QSZ + Q_B, [[512, 64], [1, 512]]), sa.h[0:64, 0:512], [sa], [qb_])
            fw.dma("pool", AP(qscr, t * QSZ + Q_D, [[512, 64], [1, 512]]), sa.h[0:64, 512:1024], [sa], [qb_])
            bk = pbr.next()
            for h in range(8):
                tr(bk.h[0:64, h * 128:(h + 1) * 128], Pb.h[:, C_IXQ + h * 64:C_IXQ + (h + 1) * 64], [Pb], [bk])
            sa = stA_r.next()
            cp(evac_eng(), sa.h[0:64, :], bk.h[0:64, :], [bk], [sa])
            fw.dma("pool", AP(qscr, t * QSZ + Q_I, [[1024, 64], [1, 1024]]), sa.h[0:64, :], [sa], [qb_])

        if dbg.get('zero_ag'):
            zt = A([128, SLOTSZ // 128], BF16)
            memset('dve', zt.h[:], 0.0, [zt])
            for t in range(16):
                if t not in TILES:
                    fw.dma('sp', AP(agin[l], t * SLOTSZ, [[SLOTSZ // 128, 128], [1, SLOTSZ // 128]]), zt.h[:], [zt], [agin_b[l][t]])
        for t in (TILES if 'p1' in PH else []):
            xt = xt_r.next()
            fw.dma("sp", xt.h[:], AP(xsrc_t, t * 128 * D, [[D, 128], [1, D]]), [xsrc_b[t]] if xsrc_b else [], [xt])
            rM = rM_r.next(); rP = rP_r.next()
            fw.dma("sp", rM.h[:], AP(ropeM, t * 128 * 128, [[128, 128], [1, 128]]), [], [rM])
            fw.dma("sp", rP.h[:], AP(ropeP, t * 128 * 144, [[144, 128], [1, 144]]), [], [rP])
            xb = xb_r.next()
            cp("pool", xb.h[:], xt.h[:], [xt], [xb])
            bk = pbr.next()
            for c in range(8):
                tr(bk.h[:, c * 128:(c + 1) * 128], xb.h[:, c * 128:(c + 1) * 128], [xb], [bk])
            xT = xT_r.next()
            cp("act", xT.h[:], bk.h[:], [bk], [xT])
            fw.dma("pool", AP(xtscr[0], t * 128 * 1024, [[1024, 128], [1, 1024]]), xT.h[:], [xT], [xtscr_b[0][t]])
            Pt = P_r.next()
            for nt in range(7):
                n = min(512, NPROJ - nt * 512)
                ps = pfr.next()
                for c in range(8):
                    mm(ps.h[:, 0:n], xT.h[:, c * 128:(c + 1) * 128], W.h[:, c, nt * 512:nt * 512 + n], c == 0, c == 7, [xT, W], [ps])
                cp(evac_eng(), Pt.h[:, nt * 512:nt * 512 + n], ps.h[:, 0:n], [ps], [Pt])
            rope(Pt, C_AQ + 64, 4, 96, 16, rM, 0)
            rope(Pt, C_KR, 1, 32, 16, rM, 0)
            rope(Pt, C_DS, 8, 64, 8, rP, 0)
            rope(Pt, C_IXQ, 9, 64, 8, rP, 0)
            memset("dve", sm.h[:, 0:1], 0.0, [sm])
            act(junk.h[:], Pt.h[:, C_CKV:C_CKV + 128], AF.Square, [Pt, sm], [junk, sm], accum=sm.h[:, 0:1])
            ts("dve", sm.h[:, 1:2], sm.h[:, 0:1], 1.0 / 128, LN_EPS, ALU.mult, ALU.add, [sm], [sm])
            act(sm.h[:, 2:3], sm.h[:, 1:2], AF.Sqrt, [sm], [sm])
            fw.op("dve", lambda e, sm=sm: e.reciprocal(out=sm.h[:, 3:4], in_=sm.h[:, 2:3]), [sm], [sm])
            stt("dve", Pt.h[:, C_CKV:C_CKV + 128], Pt.h[:, C_CKV:C_CKV + 128], sm.h[:, 3:4], gk.h[:], ALU.mult, ALU.mult, [Pt, sm, gk], [Pt])
            cp("dve", ixw.h[:, t, :], Pt.h[:, C_IXW:C_IXW + 8], [Pt], [ixw])
            ob = Buf()
            fw.dma("pool", AP(o_lat, (l * NT + t) * 128 * 160, [[160, 128], [1, 160]]), Pt.h[:, C_CKV:C_CKV + 160], [Pt], [ob])
            fw.dma("pool", AP(o_sb, (l * NT + t) * 128 * 512, [[512, 128], [1, 512]]), Pt.h[:, C_SB + 256:C_SB + 768], [Pt], [ob])
            fw.dma("pool", AP(o_ds, (l * NT + t) * 128 * 512, [[512, 128], [1, 512]]), Pt.h[:, C_DS + 256:C_DS + 768], [Pt], [ob])
            fw.dma("pool", AP(o_ix, (l * NT + t) * 128 * 64, [[64, 128], [1, 64]]), Pt.h[:, C_IXK:C_IXK + 64], [Pt], [ob])
            if t >= 15:
                fw.dma("pool", AP(o_bd, (l * 2 + t - 15) * 128 * 512, [[512, 128], [1, 512]]), Pt.h[:, C_BD + 256:C_BD + 768], [Pt], [ob])
            Pb = Pb_r.next()
            cp("act", Pb.h[:, 0:1716], Pt.h[:, 0:1716], [Pt], [Pb])
            cp("pool", Pb.h[:, 1716:NPROJ], Pt.h[:, 1716:NPROJ], [Pt], [Pb])
            emit_q(Pb, t)
            if t < 16:
                emit_kv(Pb, ("ag", t), True)
            else:
                emit_kv(Pb, ("s", 32, 4), True)

        for j in (range(PAST_TILES) if 'p1' in PH else []):
            Pt = P_r.next()
            with_bd = j >= 28
            fw.dma("sp", Pt.h[:, C_CKV:C_CKV + 160], AP(c_lat, (l * PAST + j * 128) * 160, [[160, 128], [1, 160]]), [], [Pt])
            fw.dma("sp", Pt.h[:, C_SB + 256:C_SB + 768], AP(c_sb, (l * PAST + j * 128) * 512, [[512, 128], [1, 512]]), [], [Pt])
            fw.dma("sp", Pt.h[:, C_DS + 256:C_DS + 768], AP(c_ds, (l * PAST + j * 128) * 512, [[512, 128], [1, 512]]), [], [Pt])
            fw.dma("sp", Pt.h[:, C_IXK:C_IXK + 64], AP(c_ix, (l * PAST + j * 128) * 64, [[64, 128], [1, 64]]), [], [Pt])
            if with_bd:
                fw.dma("sp", Pt.h[:, C_BD + 256:C_BD + 768], AP(c_bd, (l * 512 + (j - 28) * 128) * 512, [[512, 128], [1, 512]]), [], [Pt])
            Pb = Pb_r.next()
            cp("act", Pb.h[:, C_CKV:C_BD + 768], Pt.h[:, C_CKV:C_BD + 768], [Pt], [Pb])
            cp("pool", Pb.h[:, C_DS + 256:C_IXK + 64], Pt.h[:, C_DS + 256:C_IXK + 64], [Pt], [Pb])
            emit_kv(Pb, ("s", j, j - 28), with_bd)

        if 'ag' in PH: fw.allgather(agin[l].ap(), agout[l].ap(), agin_b[l], [agout_b[l]])

        fw.barrier()
        A.reset(mark0)
        OT = A([128, NT * 8, 128], BF16)
        OT_b = [Buf() for _ in range(NT)]
        score = A([128, 16384], F32)
        junkb = A([128, 4096], BF16)
        kt_r = Ring([A([96, 8, 4, 128], BF16) for _ in range(2)])
        v_r = Ring([A([128, 8, VW], BF16) for _ in range(2)])
        kti_r = Ring([A([64, 8, 128], BF16) for _ in range(2)])
        qm_r = Ring([A([96, 512], BF16) for _ in range(1)])
        qs_r = Ring([A([64, 512], BF16) for _ in range(1)])
        qb_r = Ring([A([64, 512], BF16) for _ in range(1)])
        qd_r = Ring([A([64, 512], BF16) for _ in range(1)])
        qi_r = Ring([A([64, 1024], BF16) for _ in range(1)])
        pT_r = Ring([A([128, 512], BF16) for _ in range(3)])
        sp_r = Ring([A([128, 512], BF16) for _ in range(2)])
        f32_r = Ring([A([128, 512], F32) for _ in range(6)])
        m01_r = Ring([A([128, 1024], BF16) for _ in range(2)])
        obt_r = Ring([A([128, 256], BF16) for _ in range(2)])
        mcc = A([128, 8, 128], BF16); msb = A([128, 8, 128], BF16)
        mscc = A([128, 1, 128], BF16); mssb = A([128, 1, 128], BF16)
        madd = A([128, 1024], F32); madds = A([128, 128], F32)
        bdb = A([128, 16 * 512], BF16); bdbs = A([128, 5 * 512], BF16)
        bs = A([128, 16], F32)
        for (tl, src) in ((mcc, m_cc), (msb, m_sb), (mscc, ms_cc), (mssb, ms_sb), (madd, dsa_madd), (madds, dsas_madd)):
            fw.dma("sp", tl.h[:], src.ap(), [], [tl])
        fw.dma("sp", bdb.h[:], AP(bd_bias, l * 128 * 8192, [[8192, 128], [1, 8192]]), [], [bdb])
        fw.dma("sp", bdbs.h[:], AP(bds_bias, l * 128 * 2560, [[2560, 128], [1, 2560]]), [], [bdbs])
        pS_r = Ring([pf[0], pf[1]])
        pC = pf[2]
        pO_r = Ring([pf[3], pf[4]])
        pI = pf[5]

        def bc4(ap2d):
            return ap2d.unsqueeze(1).broadcast_to([128, 4, 128])

        for t in (TILES if 'p3' in PH else []):
            smp = t == 16
            if not smp:
                groups = [("ag", g, 8) for g in range(t + 1)]
            else:
                groups = [("s", 8 * g, 8) for g in range(4)] + [("s", 32, 1)]
            ngr = len(groups)
            mask_cc = mscc if smp else mcc
            mask_sb = mssb if smp else msb
            src_buf = skv_b[l] if smp else agout_b[l]
            qm = qm_r.next(); qs = qs_r.next(); qb = qb_r.next(); qd = qd_r.next(); qi = qi_r.next()
            fw.dma("sp", qm.h[:], AP(qscr, t * QSZ + Q_M, [[512, 96], [1, 512]]), [qscr_b[t]], [qm])
            fw.dma("sp", qs.h[:], AP(qscr, t * QSZ + Q_S, [[512, 64], [1, 512]]), [qscr_b[t]], [qs])
            fw.dma("sp", qb.h[:], AP(qscr, t * QSZ + Q_B, [[512, 64], [1, 512]]), [qscr_b[t]], [qb])
            fw.dma("sp", qd.h[:], AP(qscr, t * QSZ + Q_D, [[512, 64], [1, 512]]), [qscr_b[t]], [qd])
            fw.dma("sp", qi.h[:], AP(qscr, t * QSZ + Q_I, [[1024, 64], [1, 1024]]), [qscr_b[t]], [qi])

            def load_group(kind, G):
                kt = kt_r.next(); v = v_r.next()
                r = KROWS[kind]
                nb = G[2]
                if G[0] == "ag":
                    fw.dma("sp", kt.h[0:r].rearrange("p b h k -> p b (h k)"), ag_kt_src(l, kind, G[1]), [src_buf], [kt])
                    fw.dma("sp", v.h[:, :, :], ag_v_src(l, kind, G[1]), [src_buf], [v])
                else:
                    fw.dma("sp", kt.h[0:r, 0:nb].rearrange("p b h k -> p (b h k)"), s_kt_src(l, kind, G[1], nb), [src_buf], [kt])
                    fw.dma("sp", v.h[:, 0:nb, :], s_v_src(l, kind, G[1], nb), [src_buf], [v])
                return kt, v

            def finalize(po, m, softmax):
                ob = obt_r.next()
                if softmax:
                    for h in range(4):
                        fw.op("dve", lambda e, h=h, bs=bs, po=po: e.reciprocal(out=bs.h[:, 8 + h:9 + h], in_=po.h[:, h * 128 + 64:h * 128 + 65]), [po], [bs])
                    for h in range(4):
                        ts("dve", ob.h[:, h * 64:(h + 1) * 64], po.h[:, h * 128:h * 128 + 64], bs.h[:, 8 + h:9 + h], None, ALU.mult, None, [po, bs], [ob])
                else:
                    cp("dve", ob.h[:].rearrange("p (h c) -> p h c", h=4), po.h[:].rearrange("p (h c) -> p h c", h=4)[:, :, 0:64], [po], [ob])
                bk = pb[1]
                tr(bk.h[:, 0:128], ob.h[:, 0:128], [ob], [bk])
                tr(bk.h[:, 128:256], ob.h[:, 128:256], [ob], [bk])
                cp("act", OT.h[:, t * 8 + 2 * m:t * 8 + 2 * m + 2, :], bk.h[:, 0:256].rearrange("p (c k) -> p c k", c=2), [bk], [OT_b[t]])

            if 'i' in MIX:
                N = sum(G[2] for G in groups) * 128
                col = 0
                for G in groups:
                    nb = G[2]
                    kti = kti_r.next()
                    if G[0] == "ag":
                        fw.dma("sp", kti.h[:, :, :], ag_kti_src(l, G[1]), [src_buf], [kti])
                    else:
                        fw.dma("sp", kti.h[:, 0:nb, :], s_kti_src(l, G[1], nb), [src_buf], [kti])
                    for c0 in range(0, nb * 128, 512):
                        n = min(512, nb * 128 - c0)
                        for j in range(8):
                            mm(pI.h[:, 0:n], qi.h[:, j * 128:(j + 1) * 128], kti.h[:].rearrange("p b k -> p (b k)")[:, c0:c0 + n], True, True, [qi, kti], [pI])
                            tmp = f32_r.next()
                            act(tmp.h[:, 0:n], pI.h[:, 0:n], AF.Relu, [pI], [tmp])
                            sc = score.h[:, col + c0:col + c0 + n]
                            if j == 0:
                                ts("dve", sc, tmp.h[:, 0:n], ixw.h[:, t, 0:1], None, ALU.mult, None, [tmp, ixw], [score])
                            else:
                                stt("dve", sc, tmp.h[:, 0:n], ixw.h[:, t, j:j + 1], sc, ALU.mult, ALU.add, [tmp, ixw, score], [score])
                    col += nb * 128
                fw.op("dve", lambda e, N=N, score=score, bs=bs: e.tensor_reduce(out=bs.h[:, 0:1], in_=score.h[:, 0:N], axis=AX.X, op=ALU.max), [score], [bs])
                fw.op("dve", lambda e, N=N, score=score, bs=bs: e.tensor_reduce(out=bs.h[:, 1:2], in_=score.h[:, 0:N], axis=AX.X, op=ALU.min), [score], [bs])
                if smp:
                    tt("dve", score.h[:, N - 128:N], score.h[:, N - 128:N], madds.h[:], ALU.add, [score, madds], [score])
                else:
                    tt("dve", score.h[:, N - 1024:N], score.h[:, N - 1024:N], madd.h[:], ALU.add, [score, madd], [score])
                nch = (N + 4095) // 4096
                for it in range(N_BISECT):
                    tt("dve", bs.h[:, 2:3], bs.h[:, 0:1], bs.h[:, 1:2], ALU.add, [bs], [bs])
                    ts("dve", bs.h[:, 2:3], bs.h[:, 2:3], 0.5, None, ALU.mult, None, [bs], [bs])
                    memset("dve", bs.h[:, 12:16], 0.0, [bs])
                    for c in range(nch):
                        n = min(4096, N - c * 4096)
                        ts("dve", junkb.h[:, 0:n], score.h[:, c * 4096:c * 4096 + n], bs.h[:, 2:3], 0.0, ALU.is_ge, ALU.add,
                           [score, bs], [junkb, bs], accum=bs.h[:, 12 + c:13 + c])
                    fw.op("dve", lambda e, bs=bs: e.reduce_sum(out=bs.h[:, 3:4], in_=bs.h[:, 12:16], axis=AX.X), [bs], [bs])
                    ts("dve", bs.h[:, 4:5], bs.h[:, 3:4], TOPK - 0.5, None, ALU.is_ge, None, [bs], [bs])
                    tt("dve", bs.h[:, 5:6], bs.h[:, 2:3], bs.h[:, 1:2], ALU.subtract, [bs], [bs])
                    stt("dve", bs.h[:, 1:2], bs.h[:, 5:6], bs.h[:, 4:5], bs.h[:, 1:2], ALU.mult, ALU.add, [bs], [bs])
                    tt("dve", bs.h[:, 5:6], bs.h[:, 0:1], bs.h[:, 2:3], ALU.subtract, [bs], [bs])
                    stt("dve", bs.h[:, 0:1], bs.h[:, 5:6], bs.h[:, 4:5], bs.h[:, 2:3], ALU.mult, ALU.add, [bs], [bs])


            def mk_blocks(glist, desc):
                seq = list(range(len(glist)))
                if desc:
                    seq = seq[::-1]
                blocks = []
                for p, gi in enumerate(seq):
                    nb = glist[gi][2]
                    bl = list(range(nb))
                    if desc:
                        bl = bl[::-1]
                    for j, b in enumerate(bl):
                        blocks.append({"p": p, "gi": gi, "b": b, "glast": j == nb - 1})
                for i, blk in enumerate(blocks):
                    blk["first"] = i == 0
                    blk["last"] = i == len(blocks) - 1
                return seq, blocks

            def run_mixer(kind, glist, desc, stages, on_group=None):
                seq, blocks = mk_blocks(glist, desc)
                loaded = {}

                def get(p):
                    if p < len(seq) and p not in loaded:
                        loaded[p] = load_group(kind, glist[seq[p]])
                    return loaded.get(p)

                get(0)
                get(1)
                ns = len(stages)
                state = {}
                for step in range(len(blocks) + ns - 1):
                    for k in range(ns):
                        i = step - k
                        if 0 <= i < len(blocks):
                            blk = blocks[i]
                            if k == 0:
                                blk["kt"], blk["v"] = get(blk["p"])
                                if on_group is not None and (i == 0 or blocks[i - 1]["p"] != blk["p"]):
                                    blk["gstate"] = on_group(blk)
                                elif on_group is not None:
                                    blk["gstate"] = blocks[i - 1]["gstate"]
                            state[i] = stages[k](blk, state.get(i))
                            if k == ns - 1:
                                state.pop(i, None)
                                if blk["glast"]:
                                    get(blk["p"] + 2)

            def pv_stage(po, blk, pT):
                v = blk["v"]
                b = blk["b"]
                for h in range(4):
                    mm(po.h[:, h * 128:h * 128 + 65], pT.h[:, h * 128:(h + 1) * 128], v.h[:, b, h * 80:h * 80 + 65],
                       blk["first"] and h == 0, blk["last"], [pT, v], [po], skip=True)

            def v4(tile_):
                return tile_.h[:].rearrange("p (h q) -> p h q", h=4)

            if 'a' in MIX:
                po = pO_r.next()
                sc_m = 96 ** -0.5
                ngl = len(groups)

                def a_s(blk, st, po=po):
                    ps = pS_r.next()
                    for h in range(4):
                        mm(ps.h[:, h * 128:(h + 1) * 128], blk["kt"].h[0:96, blk["b"], h, :], qm.h[:, h * 128:(h + 1) * 128], True, True, [blk["kt"], qm], [ps])
                    pT = pT_r.next()
                    act(pT.h[:], ps.h[:], AF.Exp, [ps], [pT], scale=sc_m)
                    if blk["gi"] == ngl - 1:
                        tt("pool", v4(pT), v4(pT), bc4(mask_cc.h[:, blk["b"], :]), ALU.mult, [pT, mask_cc], [pT])
                    return pT

                run_mixer("m", groups, False, [a_s, lambda blk, pT, po=po: pv_stage(po, blk, pT)])
                finalize(po, 0, True)

            if 'b' in MIX:
                po = pO_r.next()
                ngl = len(groups)

                def b_s1(blk, st):
                    ps = pS_r.next()
                    for h in range(4):
                        mm(ps.h[:, h * 128:(h + 1) * 128], blk["kt"].h[0:64, blk["b"], h, :], qs.h[:, h * 128:(h + 1) * 128], True, True, [blk["kt"], qs], [ps])
                    zs = f32_r.next()
                    ts("dve", zs.h[:], ps.h[:], 0.125, None, ALU.mult, None, [ps], [zs])
                    e1 = f32_r.next()
                    act(e1.h[:], zs.h[:], AF.Exp, [zs], [e1])
                    sp = sp_r.next()
                    act(sp.h[:], e1.h[:], AF.Ln, [e1], [sp], bias=1.0)
                    if blk["gi"] == ngl - 1:
                        tt("dve", v4(sp), v4(sp), bc4(mask_sb.h[:, blk["b"], :]), ALU.mult, [sp, mask_sb], [sp])
                    return (zs, sp)

                def b_s2(blk, st):
                    zs, sp = st
                    mm(pC.h[:], tri, sp.h[:], blk["first"], True, [cm, sp], [pC], skip=True)
                    u = f32_r.next()
                    tt("dve", u.h[:], zs.h[:], pC.h[:], ALU.subtract, [zs, pC], [u])
                    mm(pC.h[:], lo_m, sp.h[:], False, True, [cm, sp], [pC], skip=True)
                    pT = pT_r.next()
                    act(pT.h[:], u.h[:], AF.Exp, [u], [pT])
                    if blk["gi"] == ngl - 1:
                        tt("dve", v4(pT), v4(pT), bc4(mask_sb.h[:, blk["b"], :]), ALU.mult, [pT, mask_sb], [pT])
                    return pT

                run_mixer("s", groups, True, [b_s1, b_s2, lambda blk, pT, po=po: pv_stage(po, blk, pT)])
                finalize(po, 1, False)

            if 'c' in MIX:
                po = pO_r.next()
                if smp:
                    bgl, boffs, btile = [("s", 0, 5)], [0], bdbs
                elif t == 0:
                    bgl, boffs, btile = [("ag", 0, 8)], [8], bdb
                else:
                    bgl, boffs, btile = [("ag", t - 1, 8), ("ag", t, 8)], [0, 8], bdb

                def c_s(blk, st, boffs=boffs, btile=btile):
                    ps = pS_r.next()
                    for h in range(4):
                        mm(ps.h[:, h * 128:(h + 1) * 128], blk["kt"].h[0:64, blk["b"], h, :], qb.h[:, h * 128:(h + 1) * 128], True, True, [blk["kt"], qb], [ps])
                    u = f32_r.next()
                    bo = boffs[blk["gi"]] + blk["b"]
                    stt("dve", u.h[:], ps.h[:], 0.125, btile.h[:, bo * 512:(bo + 1) * 512], ALU.mult, ALU.add, [ps, btile], [u])
                    pT = pT_r.next()
                    act(pT.h[:], u.h[:], AF.Exp, [u], [pT])
                    return pT

                run_mixer("b", bgl, False, [c_s, lambda blk, pT, po=po: pv_stage(po, blk, pT)])
                finalize(po, 2, True)

            if 'd' in MIX:
                po = pO_r.next()
                gcol = []
                c_ = 0
                for G in groups:
                    gcol.append(c_)
                    c_ += G[2] * 128
                mk_r = Ring([pb[0], pb[1]])

                def d_group(blk):
                    G = groups[blk["gi"]]
                    nb = G[2]
                    col = gcol[blk["gi"]]
                    m01 = m01_r.next()
                    ts("dve", m01.h[:, 0:nb * 128], score.h[:, col:col + nb * 128], bs.h[:, 1:2], None, ALU.is_ge, None, [score, bs], [m01])
                    mk = mk_r.next()
                    for b in range(nb):
                        tr(mk.h[:, b * 128:(b + 1) * 128], m01.h[:, b * 128:(b + 1) * 128], [m01], [mk])
                    return mk

                def d_s(blk, st):
                    ps = pS_r.next()
                    for h in range(4):
                        mm(ps.h[:, h * 128:(h + 1) * 128], blk["kt"].h[0:64, blk["b"], h, :], qd.h[:, h * 128:(h + 1) * 128], True, True, [blk["kt"], qd], [ps])
                    pT = pT_r.next()
                    act(pT.h[:], ps.h[:], AF.Exp, [ps], [pT], scale=0.125)
                    mk = blk["gstate"]
                    tt("dve", v4(pT), v4(pT), bc4(mk.h[:, blk["b"] * 128:(blk["b"] + 1) * 128]), ALU.mult, [pT, mk], [pT])
                    return pT

                run_mixer("d", groups, False, [d_s, lambda blk, pT, po=po: pv_stage(po, blk, pT)], on_group=d_group)
                finalize(po, 3, True)

        if dbg.get('dump_ot') and l == 0:
            dbg_ot = dout('dbg_ot', [NT, 128, 1024], BF16)
            for t in (TILES if 'p3' in PH else []):
                fw.dma('pool', AP(dbg_ot, t * 128 * 1024, [[1024, 128], [1, 1024]]), OT.h[:, t * 8:(t + 1) * 8, :].rearrange('p c k -> p (c k)'), [OT_b[t]], [Buf()])
        fw.barrier()
        mark_a = A.mark()
        A.reset(mark0)
        OT2 = A([128, NT * 8, 128], BF16)
        OT2.b = None
        stg = Ring([A([128, 1024], F32) for _ in range(2)])
        Wg = A([128, 8, 4096], BF16)
        Wb = A([128, 8, 1024], BF16)
        Wo = A([128, 8, 1024], BF16)
        g1 = A([128, D], F32); b1 = A([128, D], F32)
        fw.dma("sp", g1.h[:], bcast_rows(ln1g, l * D, D), [], [g1])
        fw.dma("sp", b1.h[:], bcast_rows(ln1b, l * D, D), [], [b1])
        load_w(Wg, w_in, l * D * 7528, 7528, 8, 4096, stg, col0=C_GATE)
        load_w(Wb, w_br, l * 1024 * D, D, 8, 1024, stg)
        load_w(Wo, w_out, l * D * D, D, 8, 1024, stg)
        xT_r = Ring([A([128, D], BF16) for _ in range(2)])
        xt_r = Ring([A([128, D], F32) for _ in range(2)])
        mg_r = Ring([A([128, D], F32) for _ in range(1)])
        mgb_r = Ring([A([128, D], BF16) for _ in range(1)])
        mgT_r = Ring([A([128, D], BF16) for _ in range(1)])
        gsb_r = Ring([A([128, 512], F32) for _ in range(3)])
        y_r = Ring([A([128, D], F32) for _ in range(1)])
        yc_r = Ring([A([128, D], F32) for _ in range(1)])
        x1_r = Ring([A([128, D], F32) for _ in range(1)])
        x1b_r = Ring([A([128, D], BF16) for _ in range(1)])
        x1T_r = Ring([A([128, D], BF16) for _ in range(1)])
        junk4 = A([128, D], BF16)
        st_r = Ring([A([128, 8], F32) for _ in range(2)])
        pg_r = Ring([pf[0], pf[1]])
        pp_r = Ring([pf[2], pf[3]])
        for t in (TILES if 'p4a' in PH else []):
            xT = xT_r.next()
            fw.dma("sp", xT.h[:], AP(xtscr[0], t * 128 * 1024, [[1024, 128], [1, 1024]]), [xtscr_b[0][t]], [xT])
            xt = xt_r.next()
            fw.dma("sp", xt.h[:], AP(xsrc_t, t * 128 * D, [[D, 128], [1, D]]), [xsrc_b[t]] if xsrc_b else [], [xt])
            mg = mg_r.next()
            for nt_ in range(2):
                for m in range(4):
                    pg = pg_r.next()
                    for c in range(8):
                        mm(pg.h[:], xT.h[:, c * 128:(c + 1) * 128], Wg.h[:, c, m * 1024 + nt_ * 512:m * 1024 + nt_ * 512 + 512], c == 0, c == 7, [xT, Wg], [pg])
                    gs = gsb_r.next()
                    act(gs.h[:], pg.h[:], AF.Sigmoid, [pg], [gs])
                    pp = pp_r.next()
                    for c in range(2):
                        mm(pp.h[:], OT.h[:, t * 8 + 2 * m + c, :], Wb.h[:, 2 * m + c, nt_ * 512:nt_ * 512 + 512], c == 0, c == 1, [OT_b[t], Wb], [pp])
                    mgs = mg.h[:, nt_ * 512:(nt_ + 1) * 512]
                    if m == 0:
                        tt("dve", mgs, gs.h[:], pp.h[:], ALU.mult, [gs, pp], [mg])
                    else:
                        tt("dve", gs.h[:], gs.h[:], pp.h[:], ALU.mult, [gs, pp], [gs])
                        tt("pool", mgs, mgs, gs.h[:], ALU.add, [gs, mg], [mg])
            mgb = mgb_r.next()
            cp("pool", mgb.h[:], mg.h[:], [mg], [mgb])
            bk = pb[0]
            for c in range(8):
                tr(bk.h[:, c * 128:(c + 1) * 128], mgb.h[:, c * 128:(c + 1) * 128], [mgb], [bk])
            mgT = mgT_r.next()
            cp("act", mgT.h[:], bk.h[:], [bk], [mgT])
            y = y_r.next()
            for nt_ in range(2):
                po_ = pf[4 + nt_]
                for c in range(8):
                    mm(po_.h[:], mgT.h[:, c * 128:(c + 1) * 128], Wo.h[:, c, nt_ * 512:(nt_ + 1) * 512], c == 0, c == 7, [mgT, Wo], [po_])
                stt("dve", y.h[:, nt_ * 512:(nt_ + 1) * 512], xt.h[:, nt_ * 512:(nt_ + 1) * 512], ALPHA, po_.h[:], ALU.mult, ALU.add, [xt, po_], [y])
            x1 = x1_r.next()
            layer_norm(y, g1, b1, x1, yc_r.next(), junk4, st_r.next())
            fw.dma("pool", AP(xres[0], t * 128 * D, [[D, 128], [1, D]]), x1.h[:], [x1], [xres_b[0][t]])
            x1b = x1b_r.next()
            cp("pool", x1b.h[:], x1.h[:], [x1], [x1b])
            bk = pb[1]
            for c in range(8):
                tr(bk.h[:, c * 128:(c + 1) * 128], x1b.h[:, c * 128:(c + 1) * 128], [x1b], [bk])
            x1T = x1T_r.next()
            cp("act", x1T.h[:], bk.h[:], [bk], [x1T])
            fw.dma("pool", AP(xtscr[1], t * 128 * 1024, [[1024, 128], [1, 1024]]), x1T.h[:], [x1T], [xtscr_b[1][t]])

        fw.barrier()
        A.reset(mark0)
        stg = Ring([A([128, 1024], F32) for _ in range(2)])
        Wgu = A([128, 8, 2 * D_FF], BF16)
        Wd = A([128, 22, 1024], BF16)
        g2 = A([128, D], F32); b2 = A([128, D], F32)
        fw.dma("sp", g2.h[:], bcast_rows(ln2g, l * D, D), [], [g2])
        fw.dma("sp", b2.h[:], bcast_rows(ln2b, l * D, D), [], [b2])
        load_w(Wgu, w_gu, l * D * 2 * D_FF, 2 * D_FF, 8, 2 * D_FF, stg)
        load_w(Wd, w_dn, l * D_FF * D, D, 22, 1024, stg)
        xT_r = Ring([A([128, D], BF16) for _ in range(2)])
        xt_r = Ring([A([128, D], F32) for _ in range(2)])
        hb_r = Ring([A([128, D_FF], BF16) for _ in range(1)])
        hT_r = Ring([A([128, 22 * 128], BF16) for _ in range(1)])
        sg_r = Ring([A([128, 512], F32) for _ in range(2)])
        y_r = Ring([A([128, D], F32) for _ in range(1)])
        yc_r = Ring([A([128, D], F32) for _ in range(1)])
        x2_r = Ring([A([128, D], F32) for _ in range(1)])
        junk4 = A([128, D], BF16)
        st_r = Ring([A([128, 8], F32) for _ in range(2)])
        pg_r = Ring([pf[0], pf[1]])
        pu_r = Ring([pf[2], pf[3]])
        for t in (TILES if 'p4b' in PH else []):
            xT = xT_r.next()
            fw.dma("sp", xT.h[:], AP(xtscr[1], t * 128 * 1024, [[1024, 128], [1, 1024]]), [xtscr_b[1][t]], [xT])
            xt = xt_r.next()
            fw.dma("sp", xt.h[:], AP(xres[0], t * 128 * D, [[D, 128], [1, D]]), [xres_b[0][t]], [xt])
            hb = hb_r.next()
            for j in range(6):
                n = min(512, D_FF - j * 512)
                pg = pg_r.next(); pu = pu_r.next()
                for c in range(8):
                    mm(pg.h[:, 0:n], xT.h[:, c * 128:(c + 1) * 128], Wgu.h[:, c, j * 512:j * 512 + n], c == 0, c == 7, [xT, Wgu], [pg])
                for c in range(8):
                    mm(pu.h[:, 0:n], xT.h[:, c * 128:(c + 1) * 128], Wgu.h[:, c, D_FF + j * 512:D_FF + j * 512 + n], c == 0, c == 7, [xT, Wgu], [pu])
                sg = sg_r.next()
                act(sg.h[:, 0:n], pg.h[:, 0:n], AF.Silu, [pg], [sg])
                tt("dve", hb.h[:, j * 512:j * 512 + n], sg.h[:, 0:n], pu.h[:, 0:n], ALU.mult, [sg, pu], [hb])
            hT = hT_r.next()
            for q0 in range(0, 22, 8):
                nq = min(8, 22 - q0)
                bk = pb[(q0 // 8) % 2]
                for c in range(nq):
                    tr(bk.h[:, c * 128:(c + 1) * 128], hb.h[:, (q0 + c) * 128:(q0 + c + 1) * 128], [hb], [bk])
                cp("act", hT.h[:, q0 * 128:(q0 + nq) * 128], bk.h[:, 0:nq * 128], [bk], [hT])
            y = y_r.next()
            for nt_ in range(2):
                po_ = pf[4 + nt_]
                for c in range(22):
                    mm(po_.h[:], hT.h[:, c * 128:(c + 1) * 128], Wd.h[:, c, nt_ * 512:(nt_ + 1) * 512], c == 0, c == 21, [hT, Wd], [po_])
                stt("dve", y.h[:, nt_ * 512:(nt_ + 1) * 512], xt.h[:, nt_ * 512:(nt_ + 1) * 512], ALPHA, po_.h[:], ALU.mult, ALU.add, [xt, po_], [y])
            x2 = x2_r.next()
            layer_norm(y, g2, b2, x2, yc_r.next(), junk4, st_r.next())
            if l == DEPTH - 1:
                fw.dma("pool", AP(y_out, t * 128 * D, [[D, 128], [1, D]]), x2.h[:], [x2], [Buf()])
            else:
                fw.dma("pool", AP(xres[1], t * 128 * D, [[D, 128], [1, D]]), x2.h[:], [x2], [xres_b[1][t]])

    fw.emit()
    return nc


def _rope_tab(pos, r, theta, nrep):
    half = r // 2
    inv = np.exp(np.arange(half, dtype=np.float32) * np.float32(-2.0 * math.log(theta) / r)).astype(np.float32)
    ang = pos.astype(np.float32)[:, None] * inv[None, :]
    cos = np.cos(ang).astype(np.float32)
    sin = np.sin(ang).astype(np.float32)
    return np.stack([np.tile(cos, (1, nrep)), np.tile(sin, (1, nrep))], axis=1)


def _core_constants(i, band_rel_bias):
    bf = ml_dtypes.bfloat16
    c = {}
    pos = np.zeros((NT, 128), np.int64)
    for s in range(16):
        pos[s] = (8 * s + i) * 128 + np.arange(128)
    pos[16] = PAST + np.arange(128)
    c["ropeM"] = _rope_tab(pos.reshape(-1), 32, 10000.0, 4).reshape(NT, 128, 2, 64).astype(np.float32)
    c["ropeP"] = _rope_tab(pos.reshape(-1), 16, 500000.0, 9).reshape(NT, 128, 2, 72).astype(np.float32)
    k = np.arange(128)[:, None]
    q = np.arange(128)[None, :]
    cm = np.zeros((128, 3, 128), np.float32)
    cm[:, 0, :] = np.eye(128)
    cm[:, 1, :] = (k >= q)
    cm[:, 2, :] = (k < q)
    c["cmat"] = cm.astype(bf)
    mcc = np.zeros((128, 8, 128), np.float32)
    msb = np.zeros((128, 8, 128), np.float32)
    madd = np.zeros((128, 1024), np.float32)
    for b in range(8):
        if b < i:
            mcc[:, b, :] = 1
            msb[:, b, :] = 1
        elif b == i:
            mcc[:, b, :] = (k // 64) <= (q // 64)
            msb[:, b, :] = k < q
            madd[:, b * 128:(b + 1) * 128] = np.where((k // 64) <= (q // 64), 0.0, NEG * 1e4).T
        else:
            madd[:, b * 128:(b + 1) * 128] = NEG * 1e4
    c["m_cc"] = mcc.astype(bf)
    c["m_sb"] = msb.astype(bf)
    c["dsa_madd"] = madd
    mscc = np.zeros((128, 1, 128), np.float32)
    mscc[:64, 0, :] = 1
    mssb = np.zeros((128, 1, 128), np.float32)
    mssb[:, 0, :] = (k < q) & (k < 64)
    c["ms_cc"] = mscc.astype(bf)
    c["ms_sb"] = mssb.astype(bf)
    ms_add = np.zeros((128, 128), np.float32)
    ms_add[:, 64:] = NEG * 1e4
    c["dsas_madd"] = ms_add
    bd = np.full((DEPTH, 128, 16, 4, 128), NEG, np.float32)
    for blk in range(16):
        delta = (i + 8 - blk) if blk < 8 else (i - (blk - 8))
        if delta < 0 or delta > 4:
            continue
        dist = q + 128 * delta - k
        qch = q // 64
        kch_rel = (k // 64) - 2 * delta
        ok = (kch_rel <= qch) & (kch_rel >= qch - 8)
        idx = np.clip(dist, -128, 128) + 128
        for l in range(DEPTH):
            for h in range(4):
                bd[l, :, blk, h, :] = np.where(ok, band_rel_bias[l, h][idx], NEG)
    c["bd_bias"] = bd.reshape(DEPTH, 128, 16 * 512).astype(bf)
    bds = np.full((DEPTH, 128, 5, 4, 128), NEG, np.float32)
    for blk in range(5):
        kk = blk * 128 + k
        dist = q + 512 - kk
        ok = (kk < 576) & (q < 64)
        idx = np.clip(dist, -128, 128) + 128
        for l in range(DEPTH):
            for h in range(4):
                bds[l, :, blk, h, :] = np.where(ok, band_rel_bias[l, h][idx], np.where(kk < 576, 0.0, NEG))
    c["bds_bias"] = bds.reshape(DEPTH, 128, 5 * 512).astype(bf)
    return c


_NC_CACHE = {}
_DBG = {}


def kernel(x_prompt, x_sample, cache_mla_latent, cache_sb_kv, cache_band_kv, cache_dsa_kv, cache_dsa_kidx,
           w_in, mla_kv_norm, mla_w_uk, mla_w_uv, band_rel_bias, w_branch, w_out, ln1_g, ln1_b,
           w_gate_up, w_down, ln2_g, ln2_b):
    f = lambda a: np.ascontiguousarray(np.asarray(a, dtype=np.float32))
    x_prompt, x_sample = f(x_prompt), f(x_sample)
    band_rel_bias = f(band_rel_bias)
    shared = {
        "w_in": f(w_in), "mla_kv_norm": f(mla_kv_norm), "mla_w_uk": f(mla_w_uk).reshape(DEPTH, 128, 256),
        "mla_w_uv": f(mla_w_uv).reshape(DEPTH, 128, 256), "w_branch": f(w_branch).reshape(DEPTH, 1024, D),
        "w_out": f(w_out), "ln1_g": f(ln1_g), "ln1_b": f(ln1_b), "w_gate_up": f(w_gate_up), "w_down": f(w_down),
        "ln2_g": f(ln2_g), "ln2_b": f(ln2_b),
    }
    c_lat, c_sb, c_bd, c_ds, c_ix = f(cache_mla_latent), f(cache_sb_kv), f(cache_band_kv), f(cache_dsa_kv), f(cache_dsa_kidx)
    xp = x_prompt[0].reshape(16, 8, 128, D)
    in_maps = []
    for i in range(NCORES):
        m = dict(shared)
        xi = np.zeros((NT, 128, D), np.float32)
        xi[:16] = xp[:, i]
        xi[16, :64] = x_sample[i]
        m["x_in"] = xi
        m["c_lat"] = np.ascontiguousarray(c_lat[:, i])
        m["c_sb"] = np.ascontiguousarray(c_sb[:, i].reshape(DEPTH, PAST, 512))
        m["c_bd"] = np.ascontiguousarray(c_bd[:, i].reshape(DEPTH, 512, 512))
        m["c_ds"] = np.ascontiguousarray(c_ds[:, i].reshape(DEPTH, PAST, 512))
        m["c_ix"] = np.ascontiguousarray(c_ix[:, i])
        m.update(_core_constants(i, band_rel_bias))
        in_maps.append(m)
    if _DBG.get('prep_only'):
        return in_maps
    if "nc" not in _NC_CACHE:
        _NC_CACHE["nc"] = build_program()
    res = run_bass_kernel_spmd(_NC_CACHE["nc"], in_maps, core_ids=list(range(NCORES)))
    R = res.results
    return _assemble(R)


def _assemble(R):
    S = 16384

    def gather_p(name, width):
        out = np.zeros((DEPTH, 1, S, width), np.float32)
        o4 = out.reshape(DEPTH, 16, 8, 128, width)
        for i in range(NCORES):
            o4[:, :, i] = R[i][name][:, :16]
        return out

    def gather_s(name, width):
        return np.stack([R[i][name][:, 16, :64] for i in range(NCORES)], axis=1)

    y_p = np.zeros((1, S, D), np.float32)
    yp4 = y_p.reshape(16, 8, 128, D)
    for i in range(NCORES):
        yp4[:, i] = R[i]["y"][:16]
    y_s = np.stack([R[i]["y"][16, :64] for i in range(NCORES)], axis=0)
    lat_p = gather_p("o_lat", 160)
    sb_p = gather_p("o_sb", 512).reshape(DEPTH, 1, S, 2, 4, 64)
    ds_p = gather_p("o_ds", 512).reshape(DEPTH, 1, S, 2, 4, 64)
    ix_p = gather_p("o_ix", 64)
    band_p = np.concatenate([R[i]["o_bd"][:, 0] for i in range(4, 8)], axis=1).reshape(DEPTH, 1, 512, 2, 4, 64)
    lat_s = gather_s("o_lat", 160)
    sb_s = gather_s("o_sb", 512).reshape(DEPTH, NCORES, 64, 2, 4, 64)
    ds_s = gather_s("o_ds", 512).reshape(DEPTH, NCORES, 64, 2, 4, 64)
    ix_s = gather_s("o_ix", 64)
    band_s = np.stack([R[i]["o_bd"][:, 1, :64] for i in range(NCORES)], axis=1).reshape(DEPTH, NCORES, 64, 2, 4, 64)
    return (y_p, y_s, lat_p, sb_p, band_p, ds_p, ix_p, lat_s, sb_s, band_s, ds_s, ix_s)
```
